# Optimizing a Trainium2 kernel written in Bass

```python
import math
import jax, jax.numpy as jnp
from jax import lax
import numpy as np

D_MODEL = 2048
BATCH = 2
SEQ = 16384
DEPTH = 1

MEM_LEN = 256
ROPE_THETA = 500000.0
EPS = 1e-6
Q_BLOCK = 128
DIFF_HEADS = 8
DIFF_D = 64
DIFF_VD = 2 * DIFF_D
DIFF_W = DIFF_HEADS * DIFF_VD
ROT_DIM = DIFF_D // 4
SGU_GROUPS = 4
SGU_GD = 128
SGU_W = SGU_GROUPS * SGU_GD
CHUNK = 128
MEM_HEADS = 4
MEM_HD = 128
MEM_W = MEM_HEADS * MEM_HD
MIX_W = DIFF_W + SGU_W + MEM_W
C_DQ = 0
C_DK = C_DQ + DIFF_W
C_DV = C_DK + DIFF_W
C_SU = C_DV + DIFF_W
C_SV = C_SU + SGU_W
C_MQ = C_SV + SGU_W
IN_COLS = C_MQ + MEM_W
D_FF = int(math.ceil(8 * D_MODEL / 3 / 256) * 256)

kernel_name = "hybrid_diffattn_sgu_memxattn_block"


def rms_norm(x, g):
    xf = x.astype(jnp.float32)
    y = xf * lax.rsqrt(jnp.mean(xf * xf, axis=-1, keepdims=True) + EPS)
    return (y * g.astype(jnp.float32)).astype(x.dtype)


def lambda_init_fn(layer_idx):
    return 0.8 - 0.6 * math.exp(-0.3 * layer_idx)


def partial_rope(t, positions):
    inv_freq = 1.0 / (ROPE_THETA ** (jnp.arange(0, ROT_DIM, 2, dtype=jnp.float32) / ROT_DIM))
    ang = positions.astype(jnp.float32)[..., None] * inv_freq
    cos = jnp.cos(ang)[:, :, None, None, :]
    sin = jnp.sin(ang)[:, :, None, None, :]
    tr = t[..., :ROT_DIM].astype(jnp.float32)
    x1, x2 = tr[..., :ROT_DIM // 2], tr[..., ROT_DIM // 2:]
    rot = jnp.concatenate([x1 * cos - x2 * sin, x2 * cos + x1 * sin], axis=-1)
    return jnp.concatenate([rot.astype(t.dtype), t[..., ROT_DIM:]], axis=-1)


def diff_attention(q, k, v, lam):
    B, S, H, _, d = q.shape
    nblk = S // Q_BLOCK
    scale = 1.0 / math.sqrt(d)
    qb = q.reshape(B, nblk, Q_BLOCK, H, 2, d).transpose(1, 0, 2, 3, 4, 5)
    kpos = jnp.arange(S)
    neg = jnp.finfo(jnp.float32).min

    def block(args):
        qi, i = args
        s = jnp.einsum('bqhcd,bkhcd->bhcqk', qi, k,
                       preferred_element_type=jnp.float32) * scale
        qpos = i * Q_BLOCK + jnp.arange(Q_BLOCK)
        mask = kpos[None, :] <= qpos[:, None]
        p = jax.nn.softmax(jnp.where(mask, s, neg), axis=-1)
        a = p[:, :, 0] - lam * p[:, :, 1]
        return jnp.einsum('bhqk,bkhe->bqhe', a.astype(v.dtype), v)

    out = lax.map(block, (qb, jnp.arange(nblk)))
    return out.transpose(1, 0, 2, 3, 4).reshape(B, S, H, v.shape[-1])


def chunked_sgu(u, v, w_s, b_s, v_norm):
    B, S, _ = u.shape
    v = rms_norm(v, v_norm)
    vc = v.reshape(B, S // CHUNK, CHUNK, SGU_GROUPS, SGU_GD)
    causal = jnp.tril(jnp.ones((CHUNK, CHUNK), dtype=bool))
    w = jnp.where(causal[None], w_s, jnp.zeros((), w_s.dtype))
    vm = jnp.einsum('gts,bnsgc->bntgc', w, vc) + b_s.T[None, None, :, :, None]
    return u * vm.reshape(B, S, SGU_W)


def mem_cross_attention(q, mem, mem_norm, w_mem_kv):
    B, M, _ = mem.shape
    kv = rms_norm(mem, mem_norm) @ w_mem_kv
    k = kv[..., :MEM_W].reshape(B, M, MEM_HEADS, MEM_HD)
    v = kv[..., MEM_W:].reshape(B, M, MEM_HEADS, MEM_HD)
    s = jnp.einsum('bshe,bmhe->bhsm', q, k,
                   preferred_element_type=jnp.float32) / math.sqrt(MEM_HD)
    p = jax.nn.softmax(s, axis=-1).astype(v.dtype)
    o = jnp.einsum('bhsm,bmhe->bshe', p, v)
    return o.reshape(q.shape[0], q.shape[1], MEM_W)


def setup_inputs(seed: int = 0) -> dict:
    key = jax.random.key(seed)
    ks = jax.random.split(key, 24)
    f32 = jnp.float32
    nrm = lambda k, shape, s: jax.random.normal(k, shape, f32) * s
    gain = lambda k, n: jnp.ones((DEPTH, n), f32) + nrm(k, (DEPTH, n), 0.02)
    x = jax.random.normal(ks[0], (BATCH, SEQ, D_MODEL), f32)
    mem = jax.random.normal(ks[1], (BATCH, MEM_LEN, D_MODEL), f32)
    offset = jax.random.randint(ks[2], (BATCH, 1), 0, 1024, dtype=jnp.int32)
    positions = offset + jnp.arange(SEQ, dtype=jnp.int32)[None, :]
    return {
        "x": x,
        "mem": mem,
        "positions": positions,
        "pre_mix_norm": gain(ks[3], D_MODEL),
        "w_in": nrm(ks[4], (DEPTH, D_MODEL, IN_COLS), D_MODEL ** -0.5),
        "lambda_q1": nrm(ks[5], (DEPTH, DIFF_D), 0.1),
        "lambda_k1": nrm(ks[6], (DEPTH, DIFF_D), 0.1),
        "lambda_q2": nrm(ks[7], (DEPTH, DIFF_D), 0.1),
        "lambda_k2": nrm(ks[8], (DEPTH, DIFF_D), 0.1),
        "diff_subln": gain(ks[9], DIFF_VD),
        "sgu_v_norm": gain(ks[10], SGU_W),
        "spatial_w": nrm(ks[11], (DEPTH, SGU_GROUPS, CHUNK, CHUNK), 0.5 * CHUNK ** -0.5),
        "spatial_b": jnp.ones((DEPTH, SGU_GROUPS, CHUNK), f32) + nrm(ks[12], (DEPTH, SGU_GROUPS, CHUNK), 0.01),
        "mem_norm": gain(ks[13], D_MODEL),
        "w_mem_kv": nrm(ks[14], (DEPTH, D_MODEL, 2 * MEM_W), D_MODEL ** -0.5),
        "w_out": nrm(ks[15], (DEPTH, MIX_W, D_MODEL), MIX_W ** -0.5),
        "post_mix_norm": gain(ks[16], D_MODEL),
        "pre_ffn_norm": gain(ks[17], D_MODEL),
        "w_gate_up": nrm(ks[18], (DEPTH, D_MODEL, 2 * D_FF), D_MODEL ** -0.5),
        "w_down": nrm(ks[19], (DEPTH, D_FF, D_MODEL), D_FF ** -0.5),
        "post_ffn_norm": gain(ks[20], D_MODEL),
    }


def reference(x, mem, positions, pre_mix_norm, w_in, lambda_q1, lambda_k1, lambda_q2, lambda_k2,
              diff_subln, sgu_v_norm, spatial_w, spatial_b, mem_norm, w_mem_kv, w_out,
              post_mix_norm, pre_ffn_norm, w_gate_up, w_down, post_ffn_norm):
    B, S, _ = x.shape
    for l in range(DEPTH):
        xn = rms_norm(x, pre_mix_norm[l])
        z = xn @ w_in[l]

        q = z[..., C_DQ:C_DK].reshape(B, S, DIFF_HEADS, 2, DIFF_D)
        k = z[..., C_DK:C_DV].reshape(B, S, DIFF_HEADS, 2, DIFF_D)
        v = z[..., C_DV:C_SU].reshape(B, S, DIFF_HEADS, DIFF_VD)
        q = partial_rope(q, positions)
        k = partial_rope(k, positions)
        lam_init = lambda_init_fn(l)
        lam = (jnp.exp(jnp.sum(lambda_q1[l].astype(jnp.float32) * lambda_k1[l].astype(jnp.float32)))
               - jnp.exp(jnp.sum(lambda_q2[l].astype(jnp.float32) * lambda_k2[l].astype(jnp.float32)))
               + lam_init)
        a_out = diff_attention(q, k, v, lam)
        a_out = rms_norm(a_out, diff_subln[l]) * (1.0 - lam_init)
        a_out = a_out.reshape(B, S, DIFF_W)

        uv = jax.nn.gelu(z[..., C_SU:C_MQ], approximate=False)
        s_out = chunked_sgu(uv[..., :SGU_W], uv[..., SGU_W:], spatial_w[l], spatial_b[l], sgu_v_norm[l])

        mq = z[..., C_MQ:IN_COLS].reshape(B, S, MEM_HEADS, MEM_HD)
        m_out = mem_cross_attention(mq, mem, mem_norm[l], w_mem_kv[l])

        mix = jnp.concatenate([a_out, s_out, m_out], axis=-1) @ w_out[l]
        x = x + rms_norm(mix, post_mix_norm[l])

        hn = rms_norm(x, pre_ffn_norm[l])
        gu = hn @ w_gate_up[l]
        f = (jax.nn.silu(gu[..., :D_FF]) * gu[..., D_FF:]) @ w_down[l]
        x = x + rms_norm(f, post_ffn_norm[l])
    return x
```

```python
import math
import numpy as np
from contextlib import ExitStack
import concourse.bass as bass
import concourse.mybir as mybir
from concourse.bass_utils import run_bass_kernel_spmd

F32 = mybir.dt.float32
BF16 = mybir.dt.bfloat16
I32 = mybir.dt.int32
AF = mybir.ActivationFunctionType
ALU = mybir.AluOpType

D = 2048
NFC = 16
INC = 4608
DFF = 5632
NFF = 44
EPS = 1e-6
LAM_INIT = 0.2
PI = math.pi
C1 = 6.28125
C2 = 2.0 * math.pi - 6.28125
INV2PI = 1.0 / (2.0 * math.pi)


class Buf:
    __slots__ = ("w", "rc", "rd", "name")

    def __init__(self, name=""):
        self.w = None
        self.rc = {}
        self.rd = []
        self.name = name


class Op:
    __slots__ = ("eng", "fn", "deps", "signal", "tok", "dma")


class Prog:
    def __init__(self, nc, es):
        self.nc = nc
        self.es = es
        self.streams = {k: [] for k in ("pe", "act", "dve", "pool", "sp")}
        self.csem = {k: es.enter_context(nc.semaphore("cs_" + k)) for k in ("pe", "act", "dve", "pool")}
        self.dma_cnt = {}
        self.dma_sems = {}
        self.nsem = 0

    def newsem(self):
        self.nsem += 1
        return self.es.enter_context(self.nc.semaphore(f"ds{self.nsem}"))

    def op(self, eng, fn, reads=(), writes=(), dma_sem=None, extra=()):
        o = Op()
        o.eng = eng
        o.fn = fn
        o.signal = False
        o.dma = dma_sem
        o.tok = None
        deps = set(extra)
        for b in reads:
            if b.w is not None:
                deps.add(b.w)
        for b in writes:
            if b.w is not None:
                deps.add(b.w)
            deps.update(b.rc.values())
            deps.update(b.rd)
        if dma_sem is None and eng == "pe":
            deps = {d for d in deps if not (d.eng == "pe" and d.dma is None)}
        o.deps = deps
        for d in deps:
            if d.dma is None:
                d.signal = True
        if dma_sem is not None:
            c = self.dma_cnt.get(id(dma_sem), 0) + 16
            self.dma_cnt[id(dma_sem)] = c
            self.dma_sems[id(dma_sem)] = dma_sem
            o.tok = (dma_sem, c)
        for b in reads:
            if dma_sem is None:
                b.rc[eng] = o
            else:
                b.rd.append(o)
        for b in writes:
            b.w = o
            b.rc = {}
            b.rd = []
        self.streams[eng].append(o)
        return o

    def A(self, eng, name, reads, writes, **kw):
        return self.op(eng, (lambda e, name=name, kw=kw: getattr(e, name)(**kw)), reads, writes)

    def dma(self, q, out, in_, reads, writes, sem, extra=()):
        return self.op(q, (lambda e, o=out, i=in_: e.dma_start(out=o, in_=i)), reads, writes, dma_sem=sem, extra=extra)

    @staticmethod
    def group(ops):
        m = max(o.tok[1] for o in ops)
        grp = set(ops)
        for o in ops:
            o.tok = (o.tok[0], m)
            o.deps = o.deps - grp

    @staticmethod
    def fence(src, dst):
        ops = set()
        for b in src:
            if b.w is not None:
                ops.add(b.w)
            ops.update(b.rc.values())
            ops.update(b.rd)
        for b in dst:
            b.rd.extend(ops)

    def emit(self, block):
        for k in ("pe", "act", "dve", "pool"):
            n = 0
            for o in self.streams[k]:
                if o.dma is None and o.signal:
                    n += 1
                    o.tok = (self.csem[k], n)

        def run(k, e):
            waited = {}
            for o in self.streams[k]:
                need = {}
                for d in o.deps:
                    sem, val = d.tok
                    if waited.get(id(sem), 0) < val and need.get(id(sem), (None, 0))[1] < val:
                        need[id(sem)] = (sem, val)
                for sid, (sem, val) in need.items():
                    e.wait_ge(sem, val)
                    waited[sid] = val
                ins = o.fn(e)
                if o.dma is not None:
                    ins.then_inc(o.dma, 16)
                elif o.signal:
                    ins.then_inc(self.csem[k], 1)

        @block.sync
        def _(e):
            run("sp", e)

        @block.tensor
        def _(e):
            run("pe", e)

        @block.scalar
        def _(e):
            run("act", e)

        @block.vector
        def _(e):
            run("dve", e)

        @block.gpsimd
        def _(e):
            run("pool", e)


class _Stop(Exception):
    pass


STOP = [99]


def ck(n):
    if n > STOP[0]:
        raise _Stop()


class Ring:
    def __init__(self, P, slots, seq, barriers=()):
        self.P = P
        self.slots = slots
        self.seq = seq
        self.n_issue = 0
        self.n_acq = 0
        self.n_rel = 0
        self.barriers = set(barriers)
        self.opened = set()

    def issue(self):
        if self.n_issue >= len(self.seq):
            return False
        if self.n_issue in self.barriers and self.n_issue not in self.opened:
            return False
        if self.n_issue - self.n_rel >= len(self.slots):
            return False
        payload, buf, sem = self.slots[self.n_issue % len(self.slots)]
        ops = []
        for (o, i, src, extra) in self.seq[self.n_issue](payload):
            ops.append(self.P.dma("sp", o, i, src, [buf], sem, extra=extra))
        Prog.group(ops)
        self.n_issue += 1
        return True

    def pump(self):
        while self.issue():
            pass

    def open_barrier(self, idx):
        self.opened.add(idx)
        self.pump()

    def acquire(self):
        assert self.n_acq < self.n_issue, "ring underflow"
        payload, buf, sem = self.slots[self.n_acq % len(self.slots)]
        self.n_acq += 1
        return payload, buf

    def release(self):
        self.n_rel += 1
        self.pump()


def build_program(SEQ):
    NB = SEQ // 128
    NCH = SEQ // 2048
    NG = NB // 4
    TQ = NCH * 512
    nc = bass.Bass("TRN2", target_bir_lowering=False)

    def din(name, shape, dt=F32):
        return nc.dram_tensor(name, shape, dt, kind="ExternalInput").ap()

    def dscr(name, shape, dt):
        return nc.dram_tensor(name, shape, dt, kind="Internal").ap()

    xq = din("xq", [TQ, D])
    xkv = din("xkv", [SEQ, D])
    posq = din("posq", [1, TQ], I32)
    poskv = din("poskv", [1, SEQ], I32)
    mem = din("mem", [256, D])
    cmat = din("cmat", [128, 7, 128])
    cpack_d = din("cpack", [128, 1024])
    spw_d = din("spw", [128, 4, 128])
    g_post = din("g_post", [1, D])
    g_postffn = din("g_postffn", [1, D])
    w_in = din("w_in", [D, INC])
    w_memkv = din("w_memkv", [D, 1024])
    w_out = din("w_out", [D, D])
    w_gu = din("w_gu", [D, 2 * DFF])
    w_down = din("w_down", [DFF, D])
    y = nc.dram_tensor("y", [TQ, D], F32, kind="ExternalOutput").ap()

    w_in_b = dscr("w_in_b", [D, INC], BF16)
    w_memkv_b = dscr("w_memkv_b", [D, 1024], BF16)
    w_out_b = dscr("w_out_b", [D, D], BF16)
    w_gu_b = dscr("w_gu_b", [D, 2 * DFF], BF16)
    w_down_b = dscr("w_down_b", [DFF, D], BF16)
    kT_d = dscr("kT_d", [8, 128, SEQ], BF16)
    v_d = dscr("v_d", [8, 128, NB, 129], BF16)
    h_d = dscr("h_d", [TQ, D], F32)

    with ExitStack() as es:
        P = Prog(nc, es)

        def sbt(name, shape, dt):
            return es.enter_context(nc.sbuf_tensor(name, shape, dt))

        cm_b = sbt("cm_b", [128, 7, 128], BF16)
        cp = sbt("cpack_t", [128, 1024], F32)
        ccol_t = cp[:, 0:4]
        gcols = cp[:, 4:52].rearrange("p (a b) -> p a b", b=16)
        spb_t = cp[:, 52:56]
        lam_t = cp[:, 64:320].rearrange("p (a b) -> p a b", b=64)
        gsub_t = cp[:, 320:448]
        gsgu_t = cp[:, 448:960]
        lamc = sbt("lamc", [128, 8], F32)
        gtile = sbt("gtile", [128, D], F32)
        wsT = sbt("wsT", [128, 4, 128], BF16)
        kmemT = sbt("kmemT", [128, 4, 256], BF16)
        vmem = sbt("vmem", [128, 2, 4, 129], BF16)
        stat = sbt("stat", [128, 64], F32)
        xs = [sbt(f"xs{i}", [128, D], F32) for i in range(2)]
        xnb = sbt("xnb", [128, D], BF16)
        cm_f = xnb[:, 0:1792].bitcast(F32).rearrange("p (a b) -> p a b", b=128)
        xT = sbt("xT", [128, 16, 512], BF16)
        AW = 35072
        arena = sbt("arena", [128, AW], F32)
        PS = [es.enter_context(nc.psum_tensor(f"ps{i}", [128, 1024], F32)) for i in range(4)]

        def bank(k):
            return PS[k // 2][:, (k % 2) * 512:(k % 2 + 1) * 512]

        def bankT(k):
            return bank(k).bitcast(BF16).rearrange("p (a b) -> p a b", b=128)

        BK = [Buf(f"bank{k}") for k in range(8)]

        def av(off, words, dt=F32, pat=None, **kw):
            a = arena[:, off:off + words]
            if dt != F32:
                a = a.bitcast(dt)
            if pat is not None:
                a = a.rearrange(pat, **kw)
            return a

        b_const = Buf("const")
        b_cmb = Buf("cm_b")
        b_lam = Buf("lam")
        b_gsub = Buf("gsub")
        b_gtile = Buf("gtile")
        b_wsT = Buf("wsT")
        b_kmem = Buf("kmem")
        b_vmem = Buf("vmem")
        b_xs = [Buf("xs0"), Buf("xs1")]
        b_xnb = Buf("xnb")
        b_xT = Buf("xT")
        b_y = Buf("y")
        b_hd = Buf("h_d")
        b_w = {k: Buf(k) for k in ("w_in_b", "w_memkv_b", "w_out_b", "w_gu_b", "w_down_b")}
        b_kv = Buf("kv_scratch")
        b_ext = Buf("ext")

        NSTAT = 16
        b_stat = [Buf(f"stat{i}") for i in range(NSTAT)]
        stat_ctr = [0]

        def stat_slot():
            i = stat_ctr[0] % NSTAT
            stat_ctr[0] += 1
            return i * 4, b_stat[i]

        sem_const = P.newsem()
        sem_xs = [P.newsem(), P.newsem()]
        sem_gt = P.newsem()
        xs_ctr = [0]

        ident = cm_b[:, 0, :]
        Pm = cm_b[:, 1, :]

        o = 0
        Wkv = av(o, 16384, BF16, "p (a b) -> p a b", b=2048); o += 16384
        xT2 = av(o, 4096, BF16, "p (a b) -> p a b", b=512); o += 4096
        kst = []; vst = []
        for i in range(2):
            kst.append(av(o, 2048, BF16, "p (a b) -> p a b", b=512)); o += 2048
        for i in range(1):
            vst.append(av(o, 2064, BF16, "p (h b e) -> p h b e", h=8, b=4)); o += 2064
        P1_ROPE = o

        def rope_layout(o):
            r = {}
            r["pos"] = [av(o, 512, I32), av(o + 512, 512, I32)]; o += 1024
            r["a"] = av(o, 512); o += 512
            r["ki"] = av(o, 512, I32); o += 512
            r["kf"] = av(o, 512); o += 512
            r["r"] = av(o, 512); o += 512
            r["m"] = r["a"]
            r["rc"] = av(o, 512); o += 512
            r["sin"] = [av(o, 512), av(o + 512, 512)]; o += 1024
            r["cos"] = [av(o, 512), av(o + 512, 512)]; o += 1024
            r["kb"] = [av(o, 256, BF16), av(o + 256, 256, BF16)]; o += 512
            r["t1"] = [av(o, 512), av(o + 512, 512)]; o += 1024
            r["t2"] = [av(o, 512), av(o + 512, 512)]; o += 1024
            return r, o

        R1, o = rope_layout(o)
        assert o <= AW, o
        b_Wkv = Buf("Wkv"); b_xT2 = Buf("xT2")
        b_kst = [Buf("kst0"), Buf("kst1")]; b_vst = [Buf("vst0")]
        sem_kst = [P.newsem(), P.newsem()]; sem_vst = [P.newsem()]
        sem_pos = [P.newsem(), P.newsem()]
        b_rope = {k: Buf("rope_" + k) for k in ("a", "ki", "kf", "r", "rc")}
        b_rope["m"] = b_rope["a"]
        b_pos = [Buf("pos0"), Buf("pos1")]
        b_tab = [Buf("tab0"), Buf("tab1")]
        b_kb = [Buf("kb0"), Buf("kb1")]
        b_t1 = [Buf("t10"), Buf("t11")]
        b_t2 = [Buf("t20"), Buf("t21")]
        rope_ctr = [0]

        def rstd_from_ss(c, sb, n):
            P.A("act", "activation", [sb], [sb], out=stat[:, c + 1:c + 2], in_=stat[:, c:c + 1], func=AF.Ln,
                scale=1.0 / n, bias=EPS)
            P.A("act", "activation", [sb], [sb], out=stat[:, c + 1:c + 2], in_=stat[:, c + 1:c + 2], func=AF.Exp,
                scale=-0.5)

        def sumsq(eng_src_ap, src_bufs, junk_ap, junk_bufs, c, sb):
            P.A("dve", "scalar_tensor_tensor", src_bufs, junk_bufs + [sb], out=junk_ap, in0=eng_src_ap, scalar=1.0,
                in1=eng_src_ap, op0=ALU.mult, op1=ALU.mult, accum_out=stat[:, c:c + 1])

        tr_ctr = [0]

        def transpose_rows(src, src_bufs, dst, dst_buf, gidx, tb=(0, 1)):
            for half in range(2):
                k = tb[half]
                bt = bankT(k)
                for j in range(8):
                    fc = half * 8 + j
                    P.A("pe", "transpose", src_bufs + [b_cmb], [BK[k]], out=bt[:, j, :],
                        in_=src[:, fc * 128:(fc + 1) * 128], identity=ident)
                if gidx is None:
                    P.A("dve", "tensor_copy", [BK[k]], [dst_buf], out=dst[:, half * 8:half * 8 + 8, :], in_=bt)
                else:
                    g = gcols[:, gidx, half * 8:half * 8 + 8].unsqueeze(2).to_broadcast([128, 8, 128])
                    P.A("dve", "tensor_tensor", [BK[k], b_const], [dst_buf], out=dst[:, half * 8:half * 8 + 8, :],
                        in0=bt, in1=g, op=ALU.mult)

        def norm_T(src_ap, src_bufs, dst, dst_buf, gidx):
            c, sb = stat_slot()
            sumsq(src_ap, src_bufs, xnb[:], [b_xnb], c, sb)
            rstd_from_ss(c, sb, D)
            P.A("dve", "tensor_scalar", src_bufs + [sb], [b_xnb], out=xnb[:], in0=src_ap, scalar1=stat[:, c + 1:c + 2],
                scalar2=None, op0=ALU.mult)
            transpose_rows(xnb, [b_xnb], dst, dst_buf, gidx)

        def load_rows(dram_rows, src_buf=b_ext, extra=()):
            i = xs_ctr[0] % 2
            xs_ctr[0] += 1
            P.dma("sp", xs[i][:], dram_rows, [src_buf], [b_xs[i]], sem_xs[i], extra=extra)
            return xs[i], b_xs[i]

        def rope_tables(pos_row, R, tb):
            pi_ = rope_ctr[0] % 2
            rope_ctr[0] += 1
            P.dma("sp", R["pos"][pi_], pos_row.partition_broadcast(128), [b_ext], [b_pos[pi_]], sem_pos[pi_])
            br = b_rope
            P.A("dve", "tensor_copy", [b_pos[pi_]], [br["a"]], out=R["a"], in_=R["pos"][pi_])
            P.A("dve", "tensor_scalar", [br["a"], b_const], [br["a"]], out=R["a"], in0=R["a"], scalar1=ccol_t[:, 0:1],
                scalar2=None, op0=ALU.mult)
            P.A("dve", "tensor_scalar", [br["a"]], [br["ki"]], out=R["ki"], in0=R["a"], scalar1=INV2PI, scalar2=None,
                op0=ALU.mult)
            P.A("dve", "tensor_copy", [br["ki"]], [br["kf"]], out=R["kf"], in_=R["ki"])
            P.A("dve", "scalar_tensor_tensor", [br["kf"], br["a"]], [br["r"]], out=R["r"], in0=R["kf"], scalar=-C1,
                in1=R["a"], op0=ALU.mult, op1=ALU.add)
            P.A("dve", "scalar_tensor_tensor", [br["kf"], br["r"]], [br["r"]], out=R["r"], in0=R["kf"], scalar=-C2,
                in1=R["r"], op0=ALU.mult, op1=ALU.add)
            P.A("dve", "tensor_scalar", [br["r"]], [br["m"]], out=R["m"], in0=R["r"], scalar1=PI / 2, scalar2=-2 * PI,
                op0=ALU.is_gt, op1=ALU.mult)
            P.A("dve", "scalar_tensor_tensor", [br["r"], br["m"]], [br["rc"]], out=R["rc"], in0=R["r"], scalar=PI / 2,
                in1=R["m"], op0=ALU.add, op1=ALU.add)
            P.A("dve", "tensor_scalar", [br["rc"]], [br["rc"]], out=R["rc"], in0=R["rc"], scalar1=-PI, scalar2=PI,
                op0=ALU.max, op1=ALU.min)
            P.A("dve", "tensor_scalar", [br["r"]], [br["r"]], out=R["r"], in0=R["r"], scalar1=-PI, scalar2=PI,
                op0=ALU.max, op1=ALU.min)
            P.A("act", "activation", [br["r"], b_const], [b_tab[tb]], out=R["sin"][tb], in_=R["r"], func=AF.Sin,
                scale=ccol_t[:, 1:2])
            P.A("act", "activation", [br["rc"]], [b_tab[tb]], out=R["cos"][tb], in_=R["rc"], func=AF.Sin)

        rp_ctr = [0]

        def rope_apply(k, R, tb, dst, dst_buf, sbanks=(4, 5)):
            i = rp_ctr[0] % 2
            rp_ctr[0] += 1
            sbk = sbanks[i]
            P.A("act", "activation", [BK[k]], [b_kb[i]], out=R["kb"][i], in_=bank(k), func=AF.Copy)
            P.A("pe", "matmul", [b_kb[i], b_cmb], [BK[sbk]], out=bank(sbk), lhsT=Pm, rhs=R["kb"][i], start=True,
                stop=True)
            P.A("dve", "tensor_tensor", [BK[sbk], b_tab[tb]], [b_t1[i]], out=R["t1"][i], in0=bank(sbk),
                in1=R["sin"][tb], op=ALU.mult)
            P.A("dve", "tensor_tensor", [BK[k], b_tab[tb]], [b_t2[i]], out=R["t2"][i], in0=bank(k), in1=R["cos"][tb],
                op=ALU.mult)
            P.A("dve", "tensor_tensor", [b_t1[i], b_t2[i]], [dst_buf], out=dst, in0=R["t1"][i], in1=R["t2"][i],
                op=ALU.add)

        sem_hd = P.newsem()
        sem_y = P.newsem()
        try:
            ck(-8)
            ops = [P.dma("sp", cm_f, cmat, [b_ext], [b_xnb], sem_const),
                   P.dma("sp", R1["t1"][0].rearrange("p (a b) -> p a b", b=128), spw_d, [b_ext], [b_t1[0]], sem_const),
                   P.dma("sp", cp[:], cpack_d, [b_ext], [b_const], sem_const)]
            Prog.group(ops)

            def cast_weight(dst, src, buf, nsplit):
                sem = P.newsem()
                rows = src.shape[0]
                step = rows // nsplit
                ops = []
                for i in range(nsplit):
                    ops.append(P.dma("pool", dst[i * step:(i + 1) * step, :], src[i * step:(i + 1) * step, :], [b_ext],
                                     [buf], sem))
                Prog.group(ops)
                tot = P.dma_cnt[id(sem)]
                P.op("pool", (lambda e, sem=sem, tot=tot: e.wait_ge(sem, tot)), [], [])

            ck(-5)
            P.A("dve", "memset", [], [b_vmem], ap=vmem[:].rearrange("p a b c -> p (a b c)"), constant=1.0)
            P.A("dve", "memset", [], [b_vst[0]], ap=vst[0].rearrange("p a b c -> p (a b c)"), constant=1.0)
            ck(-4)
            import os
            kc = os.environ.get("KCAST", "01234")
            if "0" in kc: cast_weight(w_memkv_b, w_memkv, b_w["w_memkv_b"], 8)
            if "1" in kc: cast_weight(w_in_b, w_in, b_w["w_in_b"], 8)
            if "2" in kc: cast_weight(w_out_b, w_out, b_w["w_out_b"], 8)
            if "3" in kc: cast_weight(w_gu_b, w_gu, b_w["w_gu_b"], 16)
            if "4" in kc: cast_weight(w_down_b, w_down, b_w["w_down_b"], 22)

            ck(-3)
            P.A("dve", "tensor_copy", [b_xnb], [b_cmb], out=cm_b[:], in_=cm_f)
            ck(-2)
            P.A("dve", "tensor_tensor", [b_const], [b_lam], out=lam_t[:, 0, :], in0=lam_t[:, 0, :], in1=lam_t[:, 1, :],
                op=ALU.mult)
            P.A("dve", "tensor_tensor", [b_const, b_lam], [b_lam], out=lam_t[:, 2, :], in0=lam_t[:, 2, :],
                in1=lam_t[:, 3, :], op=ALU.mult)
            P.A("act", "activation", [b_lam], [b_lam], out=lam_t[:, 1, :], in_=lam_t[:, 0, :], func=AF.Copy,
                accum_out=lamc[:, 0:1])
            P.A("act", "activation", [b_lam], [b_lam], out=lam_t[:, 3, :], in_=lam_t[:, 2, :], func=AF.Copy,
                accum_out=lamc[:, 1:2])
            P.A("act", "activation", [b_lam], [b_lam], out=lamc[:, 2:4], in_=lamc[:, 0:2], func=AF.Exp)
            P.A("dve", "tensor_tensor", [b_lam], [b_lam], out=lamc[:, 4:5], in0=lamc[:, 2:3], in1=lamc[:, 3:4],
                op=ALU.subtract)
            P.A("dve", "tensor_scalar", [b_lam], [b_lam], out=lamc[:, 5:6], in0=lamc[:, 4:5], scalar1=-1.0,
                scalar2=-LAM_INIT, op0=ALU.mult, op1=ALU.add)
            P.A("dve", "tensor_scalar", [b_const], [b_gsub], out=gsub_t[:], in0=gsub_t[:], scalar1=1.0 - LAM_INIT,
                scalar2=None, op0=ALU.mult)
            ck(-1)
            wtmp = R1["kb"][0].rearrange("p (a b) -> p a b", b=128)
            P.A("dve", "tensor_tensor", [b_xnb, b_t1[0]], [b_kb[0]], out=wtmp,
                in0=R1["t1"][0].rearrange("p (a b) -> p a b", b=128),
                in1=cm_f[:, 6, :].unsqueeze(1).to_broadcast([128, 4, 128]), op=ALU.mult)
            for g in range(4):
                P.A("pe", "transpose", [b_kb[0], b_cmb], [BK[0]], out=bankT(0)[:, g, :], in_=wtmp[:, g, :],
                    identity=ident)
            P.A("dve", "tensor_copy", [BK[0]], [b_wsT], out=wsT[:], in_=bankT(0)[:, 0:4, :])

            ck(1)
            wmk = av(0, 8192, BF16, "p (a b) -> p a b", b=1024)
            sem_wkv = P.newsem()
            ops = []
            for j in range(4):
                ops.append(P.dma("sp", wmk[:, :, j * 256:(j + 1) * 256],
                                 w_memkv_b[:, j * 256:(j + 1) * 256].rearrange("(fc p) c -> p fc c", p=128),
                                 [b_w["w_memkv_b"]], [b_Wkv], sem_wkv))
            Prog.group(ops)
            for mb in range(2):
                st, sb_ = load_rows(mem[mb * 128:(mb + 1) * 128, :])
                norm_T(st[:], [sb_], xT[:, :, mb * 128:(mb + 1) * 128], b_xT, 1)
            for hm in range(4):
                k = 2 + hm % 2
                for fc in range(16):
                    P.A("pe", "matmul", [b_Wkv, b_xT], [BK[k]], out=bank(k)[:, 0:256],
                        lhsT=wmk[:, fc, hm * 128:(hm + 1) * 128], rhs=xT[:, fc, 0:256], start=(fc == 0), stop=(fc == 15))
                P.A("act", "activation", [BK[k]], [b_kmem], out=kmemT[:, hm, :], in_=bank(k)[:, 0:256], func=AF.Copy)
            for mb in range(2):
                k = 6 + mb
                for fc in range(16):
                    P.A("pe", "matmul", [b_Wkv, b_xT], [BK[k]], out=bank(k), lhsT=xT[:, fc, mb * 128:(mb + 1) * 128],
                        rhs=wmk[:, fc, 512:1024], start=(fc == 0), stop=(fc == 15))
                P.A("act", "activation", [BK[k]], [b_vmem], out=vmem[:, mb, :, 0:128],
                    in_=bank(k).rearrange("p (a b) -> p a b", b=128), func=AF.Copy)

            ck(2)
            ops = []
            for j in range(8):
                ops.append(P.dma("sp", Wkv[:, :, j * 256:(j + 1) * 256],
                                 w_in_b[:, 1024 + j * 256:1024 + (j + 1) * 256].rearrange("(fc p) c -> p fc c", p=128),
                                 [b_w["w_in_b"]], [b_Wkv], sem_wkv))

            ck(3)
            kv_store_ops = []
            xTs = [xT, xT2]
            b_xTs = [b_xT, b_xT2]
            for g in range(NG):
                tb = g % 2
                xg = xTs[g % 2]
                bxg = b_xTs[g % 2]
                rope_tables(poskv[0:1, g * 512:(g + 1) * 512], R1, tb)
                for blk in range(4):
                    st, sb_ = load_rows(xkv[(g * 4 + blk) * 128:(g * 4 + blk + 1) * 128, :])
                    norm_T(st[:], [sb_], xg[:, :, blk * 128:(blk + 1) * 128], bxg, 0)
                ks = g % 2
                for h in range(8):
                    k = 2 + h % 2
                    for fc in range(16):
                        P.A("pe", "matmul", [b_Wkv, bxg], [BK[k]], out=bank(k), lhsT=Wkv[:, fc, h * 128:(h + 1) * 128],
                            rhs=xg[:, fc, :], start=(fc == 0), stop=(fc == 15))
                    rope_apply(k, R1, tb, kst[ks][:, h, :], b_kst[ks])
                for blk in range(4):
                    for half in range(2):
                        k = 6 + (blk * 2 + half) % 2
                        for fc in range(16):
                            P.A("pe", "matmul", [b_Wkv, bxg], [BK[k]], out=bank(k),
                                lhsT=xg[:, fc, blk * 128:(blk + 1) * 128],
                                rhs=Wkv[:, fc, 1024 + half * 512:1024 + (half + 1) * 512], start=(fc == 0),
                                stop=(fc == 15))
                        P.A("act", "activation", [BK[k]], [b_vst[0]], out=vst[0][:, half * 4:(half + 1) * 4, blk, 0:128],
                            in_=bank(k).rearrange("p (a b) -> p a b", b=128), func=AF.Copy)
                for hh in range(4):
                    hs = slice(hh * 2, hh * 2 + 2)
                    kv_store_ops.append(P.dma("pool", kT_d[hs, :, g * 512:(g + 1) * 512].rearrange("h p t -> p h t"),
                                              kst[ks][:, hs, :], [b_kst[ks]], [], sem_kst[ks]))
                    kv_store_ops.append(P.dma("pool", v_d[hs, :, g * 4:(g + 1) * 4, :].rearrange("h p b e -> p h b e"),
                                              vst[0][:, hs, :, :], [b_vst[0]], [], sem_vst[0]))
                Prog.group(kv_store_ops[-8::2])
                Prog.group(kv_store_ops[-7::2])

            ck(4)
            NRING = 4
            o = 0
            ring_views = []
            for i in range(NRING):
                ring_views.append(av(o, 2048, BF16)); o += 2048
            regA = o
            qT = av(o, 2048, BF16, "p (a b) -> p a b", b=512); o += 2048
            mqT = av(o, 1024, BF16, "p (a b) -> p a b", b=512); o += 1024
            ao_t = [av(o + i * 256, 256, BF16, "p (a b) -> p a b", b=128) for i in range(2)]; o += 512
            af_t = [av(o + i * 128, 128) for i in range(2)]; o += 256
            tf_t = [av(o + i * 128, 128) for i in range(2)]; o += 256
            jk_t = av(o, 128); o += 128
            uo = o
            u_t = av(uo, 2048, F32, "p (a b) -> p a b", b=512)
            v_t = av(uo + 2048, 2048, F32, "p (a b) -> p a b", b=512)
            vn_t = av(uo + 4096, 1024, BF16, "p (a b) -> p a b", b=512)
            so_t = av(uo + 5120, 1024, BF16, "p (a b) -> p a b", b=512)
            R2, o = rope_layout(uo)
            regA_end = o
            actT = av(regA, 11264, BF16, "p (a b) -> p a b", b=512)
            assert regA + 11264 <= AW
            o = max(o, regA + 11264)
            regB = o
            kvK = []; kvV = []
            for i in range(4):
                kvK.append(av(o, 1024, BF16)); o += 1024
                kvV.append(av(o, 1032, BF16, "p (a b) -> p a b", b=129)); o += 1032
            pT = []
            for i in range(3):
                pT.append(av(o, 512, BF16, "p (a b) -> p a b", b=512)); o += 512
            res = av(regB, 8192, F32, "p (a b) -> p a b", b=2048)
            o = max(o, regB + 8192)
            assert o <= AW, o

            b_ring = [Buf(f"ring{i}") for i in range(NRING)]
            b_qT = [Buf(f"qT{h}") for h in range(8)]
            b_mqT = [Buf(f"mqT{h}") for h in range(4)]
            b_u = Buf("u"); b_v = Buf("v"); b_vn = Buf("vn"); b_so = Buf("so")
            b_ao = [Buf("ao0"), Buf("ao1")]; b_af = [Buf("af0"), Buf("af1")]; b_tf = [Buf("tf0"), Buf("tf1")]
            b_jk = Buf("jk")
            b_actT = Buf("actT")
            b_kvs = [Buf(f"kvs{i}") for i in range(4)]
            b_pT = [Buf(f"pT{i}") for i in range(3)]
            b_res = [Buf(f"res{i}") for i in range(4)]
            p1_bufs = [b_Wkv, b_xT2] + b_kst + b_vst + list(b_rope.values()) + b_pos + b_tab + b_kb + b_t1 + b_t2
            regA_bufs = b_qT + b_mqT + [b_u, b_v, b_vn, b_so, b_jk] + b_ao + b_af + b_tf
            rope2 = {
                "rope": {k: Buf("rope2_" + k) for k in ("a", "ki", "kf", "r", "rc")},
                "pos": [Buf("q_pos0"), Buf("q_pos1")], "tab": [Buf("q_tab0"), Buf("q_tab1")],
                "kb": [Buf("q_kb0"), Buf("q_kb1")], "t1": [Buf("q_t10"), Buf("q_t11")], "t2": [Buf("q_t20"), Buf("q_t21")],
            }
            rope2["rope"]["m"] = rope2["rope"]["a"]
            regA_bufs += list(rope2["rope"].values()) + rope2["pos"] + rope2["tab"] + rope2["kb"] + rope2["t1"] + rope2["t2"]
            regB_bufs = b_kvs + b_pT
            p2_all = b_ring + regA_bufs + regB_bufs + b_res + [b_actT]
            Prog.fence(p1_bufs, p2_all)
            b_rope.update(rope2["rope"])
            b_pos[:] = rope2["pos"]; b_tab[:] = rope2["tab"]; b_kb[:] = rope2["kb"]
            b_t1[:] = rope2["t1"]; b_t2[:] = rope2["t2"]

            def wpiece(src_ap, buf, nfc, ncols):
                def f(view):
                    dst = view[:, 0:nfc * ncols].rearrange("p (a b) -> p a b", b=ncols)
                    return [(dst, src_ap.rearrange("(fc p) c -> p fc c", p=128), [buf], ())]
                return f

            wseq = []
            for s in range(NCH):
                for c0 in list(range(0, 1024, 256)) + list(range(3072, 4608, 256)):
                    wseq.append(wpiece(w_in_b[:, c0:c0 + 256], b_w["w_in_b"], 16, 256))
                for pc in range(8):
                    wseq.append(wpiece(w_out_b[:, pc * 256:(pc + 1) * 256], b_w["w_out_b"], 16, 256))
                for i in range(22):
                    wseq.append(wpiece(w_gu_b[:, i * 256:(i + 1) * 256], b_w["w_gu_b"], 16, 256))
                    wseq.append(wpiece(w_gu_b[:, DFF + i * 256:DFF + (i + 1) * 256], b_w["w_gu_b"], 16, 256))
                for sl in range(8):
                    for q in range(4):
                        wseq.append(wpiece(w_down_b[q * 1408:(q + 1) * 1408, sl * 256:(sl + 1) * 256], b_w["w_down_b"],
                                           11, 256))
            sem_ring = [P.newsem() for _ in range(NRING)]
            wring = Ring(P, [(ring_views[i], b_ring[i], sem_ring[i]) for i in range(NRING)], wseq)

            def wget(nfc, ncols):
                view, buf = wring.acquire()
                return view[:, 0:nfc * ncols].rearrange("p (a b) -> p a b", b=ncols), buf

            kvseq = []
            first_kv = [True]

            def kvpiece(h, pi):
                def f(payload):
                    K, V = payload
                    extra = tuple(kv_store_ops) if first_kv[0] else ()
                    first_kv[0] = False
                    return [(K, kT_d[h, :, pi * 2048:(pi + 1) * 2048], [b_kv], extra),
                            (V, v_d[h, :, pi * 16:(pi + 1) * 16, :], [b_kv], extra)]
                return f

            kv_chunk_start = []
            for s in range(NCH):
                kv_chunk_start.append(len(kvseq))
                for h in range(8):
                    for pi in range(s + 1):
                        kvseq.append(kvpiece(h, pi))
            sem_kvs = [P.newsem() for _ in range(4)]
            kvring = Ring(P, [((kvK[i], kvV[i]), b_kvs[i], sem_kvs[i]) for i in range(4)], kvseq,
                          barriers=kv_chunk_start[1:])

            wring.pump()
            kvring.pump()

            pT_ctr = [0]
            st_ctr = [0]
            ep_ctr = [0]

            def oreg(comp, qb):
                r = comp * 4 + qb
                k = 4 + r // 3
                c0 = (r % 3) * 129
                return bank(k)[:, c0:c0 + 129], BK[k]

            for s in range(NCH):
                tok0 = s * 512
                rope_tables(posq[0:1, tok0:tok0 + 512], R2, 0)
                for blk in range(4):
                    st, sb_ = load_rows(xq[tok0 + blk * 128:tok0 + (blk + 1) * 128, :])
                    norm_T(st[:], [sb_], xT[:, :, blk * 128:(blk + 1) * 128], b_xT, 0)
                ck(5)
                for pc in range(4):
                    wv, wb = wget(16, 256)
                    for j in range(2):
                        h = pc * 2 + j
                        k = 2 + h % 2
                        for fc in range(16):
                            P.A("pe", "matmul", [wb, b_xT], [BK[k]], out=bank(k), lhsT=wv[:, fc, j * 128:(j + 1) * 128],
                                rhs=xT[:, fc, :], start=(fc == 0), stop=(fc == 15))
                        rope_apply(k, R2, 0, qT[:, h, :], b_qT[h])
                    wring.release()
                Prog.fence(list(b_rope.values()) + b_pos + b_tab + b_kb + b_t1 + b_t2, [b_u, b_v, b_vn, b_so])
                for which in range(2):
                    dst_t, dst_b = (u_t, b_u) if which == 0 else (v_t, b_v)
                    for pc in range(2):
                        wv, wb = wget(16, 256)
                        for blk in range(4):
                            for fc in range(16):
                                P.A("pe", "matmul", [wb, b_xT], [BK[6], BK[7]], out=PS[3][:, blk * 256:(blk + 1) * 256],
                                    lhsT=xT[:, fc, blk * 128:(blk + 1) * 128], rhs=wv[:, fc, :], start=(fc == 0),
                                    stop=(fc == 15))
                        P.A("act", "activation", [BK[6], BK[7]], [dst_b], out=dst_t[:, :, pc * 256:(pc + 1) * 256],
                            in_=PS[3][:, :].rearrange("p (a b) -> p a b", b=256), func=AF.Gelu)
                        wring.release()
                for pc in range(2):
                    wv, wb = wget(16, 256)
                    for j in range(2):
                        hm = pc * 2 + j
                        k = 2 + hm % 2
                        for fc in range(16):
                            P.A("pe", "matmul", [wb, b_xT], [BK[k]], out=bank(k), lhsT=wv[:, fc, j * 128:(j + 1) * 128],
                                rhs=xT[:, fc, :], start=(fc == 0), stop=(fc == 15))
                        P.A("act", "activation", [BK[k]], [b_mqT[hm]], out=mqT[:, hm, :], in_=bank(k), func=AF.Copy)
                    wring.release()

                ck(6)
                for blk in range(4):
                    c, sb = stat_slot()
                    sumsq(v_t[:, blk, :], [b_v], vn_t[:, blk, :], [b_vn], c, sb)
                    rstd_from_ss(c, sb, 512)
                    P.A("dve", "scalar_tensor_tensor", [b_v, sb, b_const], [b_vn], out=vn_t[:, blk, :], in0=v_t[:, blk, :],
                        scalar=stat[:, c + 1:c + 2], in1=gsgu_t[:], op0=ALU.mult, op1=ALU.mult)
                    k = 2 + blk % 2
                    for g in range(4):
                        P.A("pe", "matmul", [b_vn, b_wsT], [BK[k]], out=bank(k)[:, g * 128:(g + 1) * 128], lhsT=wsT[:, g, :],
                            rhs=vn_t[:, blk, g * 128:(g + 1) * 128], start=True, stop=True)
                    for g in range(4):
                        P.A("dve", "scalar_tensor_tensor", [BK[k], b_u, b_const], [b_so],
                            out=so_t[:, blk, g * 128:(g + 1) * 128], in0=bank(k)[:, g * 128:(g + 1) * 128],
                            scalar=spb_t[:, g:g + 1], in1=u_t[:, blk, g * 128:(g + 1) * 128], op0=ALU.add, op1=ALU.mult)
                for g in range(4):
                    k = g % 2
                    for blk in range(4):
                        P.A("pe", "transpose", [b_so, b_cmb], [BK[k]], out=bankT(k)[:, blk, :],
                            in_=so_t[:, blk, g * 128:(g + 1) * 128], identity=ident)
                    P.A("dve", "tensor_copy", [BK[k]], [b_xT], out=xT[:, 8 + g, :], in_=bank(k).bitcast(BF16)[:, 0:512])

                ck(7)
                for hm in range(4):
                    pa = hm % 2
                    for mb in range(2):
                        P.A("pe", "matmul", [b_kmem, b_mqT[hm]], [BK[2 * pa + mb]], out=bank(2 * pa + mb),
                            lhsT=kmemT[:, hm, mb * 128:(mb + 1) * 128], rhs=mqT[:, hm, :], start=True, stop=True)
                    pi_ = pT_ctr[0] % 3
                    pT_ctr[0] += 1
                    P.A("act", "activation", [BK[2 * pa], BK[2 * pa + 1]], [b_pT[pi_]], out=pT[pi_],
                        in_=PS[pa][:, :].rearrange("p (a b) -> p a b", b=512), func=AF.Exp, scale=1.0 / math.sqrt(128.0))
                    for qb in range(4):
                        oap, ob = oreg(0, qb)
                        for mb in range(2):
                            P.A("pe", "matmul", [b_pT[pi_], b_vmem], [ob], out=oap,
                                lhsT=pT[pi_][:, mb, qb * 128:(qb + 1) * 128], rhs=vmem[:, mb, hm, :], start=(mb == 0),
                                stop=(mb == 1))
                    ai = ep_ctr[0] % 2
                    ep_ctr[0] += 1
                    for qb in range(4):
                        oap, ob = oreg(0, qb)
                        c, sb = stat_slot()
                        P.A("dve", "reciprocal", [ob], [sb], out=stat[:, c:c + 1], in_=oap[:, 128:129])
                        P.A("dve", "tensor_scalar", [ob, sb], [b_ao[ai]], out=ao_t[ai][:, qb, :], in0=oap[:, 0:128],
                            scalar1=stat[:, c:c + 1], scalar2=None, op0=ALU.mult)
                    for qb in range(4):
                        P.A("pe", "transpose", [b_ao[ai], b_cmb], [BK[7]], out=bankT(7)[:, qb, :], in_=ao_t[ai][:, qb, :],
                            identity=ident)
                    P.A("dve", "tensor_copy", [BK[7]], [b_xT], out=xT[:, 12 + hm, :], in_=bank(7).bitcast(BF16)[:, 0:512])

                ck(8)
                tiles = []
                for h in range(8):
                    for pi in range(s + 1):
                        for j in range(16):
                            tiles.append((h, pi, j))
                kv_cur = {}

                def kv_slot(h, pi):
                    key = (h, pi)
                    if key not in kv_cur:
                        kv_cur[key] = kvring.acquire()
                    return kv_cur[key]

                def QK(t):
                    h, pi, j = tiles[t]
                    (K, V), kb_ = kv_slot(h, pi)
                    r0 = (j // 4) if pi == s else 0
                    pa = t % 2
                    for comp in range(2):
                        P.A("pe", "matmul", [kb_, b_qT[h]], [BK[2 * pa + comp]], out=bank(2 * pa + comp)[:, r0 * 128:512],
                            lhsT=K[comp * 64:(comp + 1) * 64, j * 128:(j + 1) * 128],
                            rhs=qT[comp * 64:(comp + 1) * 64, h, r0 * 128:512], start=True, stop=True)

                def EXP_PV(t):
                    h, pi, j = tiles[t]
                    (K, V), kb_ = kv_slot(h, pi)
                    win = (pi == s)
                    r0 = (j // 4) if win else 0
                    pa = t % 2
                    pi_ = pT_ctr[0] % 3
                    pT_ctr[0] += 1
                    P.A("act", "activation", [BK[2 * pa], BK[2 * pa + 1]], [b_pT[pi_]], out=pT[pi_][:, :, r0 * 128:512],
                        in_=PS[pa][:, :].rearrange("p (a b) -> p a b", b=512)[:, :, r0 * 128:512], func=AF.Exp,
                        scale=0.125)
                    if win:
                        for comp in range(2):
                            P.A("dve", "tensor_tensor", [b_pT[pi_], b_cmb], [b_pT[pi_]],
                                out=pT[pi_][:, comp, r0 * 128:(r0 + 1) * 128], in0=pT[pi_][:, comp, r0 * 128:(r0 + 1) * 128],
                                in1=cm_b[:, 2 + j % 4, :], op=ALU.mult)
                    first = (pi == 0 and j == 0)
                    for comp in range(2):
                        for qb in range(r0, 4):
                            oap, ob = oreg(comp, qb)
                            last = win and (j == 4 * qb + 3)
                            st_ = first and ((comp * 4 + qb) % 3 == 0)
                            P.A("pe", "matmul", [b_pT[pi_], kb_], [ob], out=oap,
                                lhsT=pT[pi_][:, comp, qb * 128:(qb + 1) * 128], rhs=V[:, j, :], start=st_, stop=last,
                                skip_group_check=True)
                    if j == 15:
                        kvring.release()
                        del kv_cur[(h, pi)]
                    if win and j == 15:
                        epilogue(h)

                def epilogue(h):
                    ai = ep_ctr[0] % 2
                    ep_ctr[0] += 1
                    for qb in range(4):
                        o0, ob0 = oreg(0, qb)
                        o1, ob1 = oreg(1, qb)
                        c, sb = stat_slot()
                        fi = qb % 2
                        P.A("dve", "reciprocal", [ob0], [sb], out=stat[:, c:c + 1], in_=o0[:, 128:129])
                        P.A("dve", "reciprocal", [ob1, sb], [sb], out=stat[:, c + 1:c + 2], in_=o1[:, 128:129])
                        P.A("dve", "tensor_tensor", [sb, b_lam], [sb], out=stat[:, c + 1:c + 2], in0=stat[:, c + 1:c + 2],
                            in1=lamc[:, 5:6], op=ALU.mult)
                        P.A("dve", "tensor_scalar", [ob1, sb], [b_tf[fi]], out=tf_t[fi], in0=o1[:, 0:128],
                            scalar1=stat[:, c + 1:c + 2], scalar2=None, op0=ALU.mult)
                        P.A("dve", "scalar_tensor_tensor", [ob0, sb, b_tf[fi]], [b_af[fi]], out=af_t[fi], in0=o0[:, 0:128],
                            scalar=stat[:, c:c + 1], in1=tf_t[fi], op0=ALU.mult, op1=ALU.add)
                        c2, sb2 = stat_slot()
                        sumsq(af_t[fi], [b_af[fi]], jk_t, [b_jk], c2, sb2)
                        rstd_from_ss(c2, sb2, 128)
                        P.A("dve", "scalar_tensor_tensor", [b_af[fi], sb2, b_gsub], [b_ao[ai]], out=ao_t[ai][:, qb, :],
                            in0=af_t[fi], scalar=stat[:, c2 + 1:c2 + 2], in1=gsub_t[:], op0=ALU.mult, op1=ALU.mult)
                    for qb in range(4):
                        P.A("pe", "transpose", [b_ao[ai], b_cmb], [BK[7]], out=bankT(7)[:, qb, :], in_=ao_t[ai][:, qb, :],
                            identity=ident)
                    P.A("dve", "tensor_copy", [BK[7]], [b_xT], out=xT[:, h, :], in_=bank(7).bitcast(BF16)[:, 0:512])

                QK(0)
                for t in range(len(tiles)):
                    if t + 1 < len(tiles):
                        QK(t + 1)
                    EXP_PV(t)

                ck(9)
                Prog.fence(regB_bufs, b_res)
                P.dma("sp", gtile[:], g_post.partition_broadcast(128), [b_ext], [b_gtile], sem_gt)
                for pc in range(8):
                    wv, wb = wget(16, 256)
                    pa = pc % 2
                    for blk in range(4):
                        for fc in range(16):
                            P.A("pe", "matmul", [wb, b_xT], [BK[2 * pa], BK[2 * pa + 1]],
                                out=PS[pa][:, blk * 256:(blk + 1) * 256], lhsT=xT[:, fc, blk * 128:(blk + 1) * 128],
                                rhs=wv[:, fc, :], start=(fc == 0), stop=(fc == 15))
                    P.A("act", "activation", [BK[2 * pa], BK[2 * pa + 1]], b_res, out=res[:, :, pc * 256:(pc + 1) * 256],
                        in_=PS[pa][:, :].rearrange("p (a b) -> p a b", b=256), func=AF.Copy)
                    wring.release()
                hd_ops = []
                for blk in range(4):
                    rows = slice(tok0 + blk * 128, tok0 + (blk + 1) * 128)
                    c, sb = stat_slot()
                    sumsq(res[:, blk, :], [b_res[blk]], xnb[:], [b_xnb], c, sb)
                    rstd_from_ss(c, sb, D)
                    st, sb_ = load_rows(xq[rows, :])
                    P.A("dve", "scalar_tensor_tensor", [b_res[blk], sb, b_gtile], [b_res[blk]], out=res[:, blk, :],
                        in0=res[:, blk, :], scalar=stat[:, c + 1:c + 2], in1=gtile[:], op0=ALU.mult, op1=ALU.mult)
                    P.A("dve", "tensor_tensor", [b_res[blk], sb_], [b_res[blk]], out=res[:, blk, :], in0=res[:, blk, :],
                        in1=st[:], op=ALU.add)
                    hd_ops.append(P.dma("pool", h_d[rows, :], res[:, blk, :], [b_res[blk]], [b_hd], sem_hd))
                    norm_T(res[:, blk, :], [b_res[blk]], xT[:, :, blk * 128:(blk + 1) * 128], b_xT, 2)
                Prog.group(hd_ops)

                ck(10)
                Prog.fence(regA_bufs, [b_actT])
                for i in range(22):
                    gv, gb = wget(16, 256)
                    uv, ub = wget(16, 256)
                    for j in range(2):
                        ffc = i * 2 + j
                        pa = ffc % 2
                        for fc in range(16):
                            P.A("pe", "matmul", [gb, b_xT], [BK[2 * pa]], out=bank(2 * pa),
                                lhsT=gv[:, fc, j * 128:(j + 1) * 128], rhs=xT[:, fc, :], start=(fc == 0), stop=(fc == 15))
                        for fc in range(16):
                            P.A("pe", "matmul", [ub, b_xT], [BK[2 * pa + 1]], out=bank(2 * pa + 1),
                                lhsT=uv[:, fc, j * 128:(j + 1) * 128], rhs=xT[:, fc, :], start=(fc == 0), stop=(fc == 15))
                        ti = st_ctr[0] % 2
                        st_ctr[0] += 1
                        sg = xs[ti][:, 0:512]
                        P.A("act", "activation", [BK[2 * pa]], [b_xs[ti]], out=sg, in_=bank(2 * pa), func=AF.Silu)
                        P.A("dve", "tensor_tensor", [BK[2 * pa + 1], b_xs[ti]], [b_actT], out=actT[:, ffc, :],
                            in0=bank(2 * pa + 1), in1=sg, op=ALU.mult)
                    wring.release()
                    wring.release()

                ck(11)
                P.dma("sp", gtile[:], g_postffn.partition_broadcast(128), [b_ext], [b_gtile], sem_gt)
                for sl in range(8):
                    pa = 2 + sl % 2
                    for q in range(4):
                        wv, wb = wget(11, 256)
                        for blk in range(4):
                            for i in range(11):
                                P.A("pe", "matmul", [wb, b_actT], [BK[2 * pa], BK[2 * pa + 1]],
                                    out=PS[pa][:, blk * 256:(blk + 1) * 256],
                                    lhsT=actT[:, q * 11 + i, blk * 128:(blk + 1) * 128], rhs=wv[:, i, :],
                                    start=(q == 0 and i == 0 and blk % 2 == 0), stop=(q == 3 and i == 10), skip_group_check=True)
                        wring.release()
                    P.A("act", "activation", [BK[2 * pa], BK[2 * pa + 1]], b_res, out=res[:, :, sl * 256:(sl + 1) * 256],
                        in_=PS[pa][:, :].rearrange("p (a b) -> p a b", b=256), func=AF.Copy)
                y_ops = []
                for blk in range(4):
                    rows = slice(tok0 + blk * 128, tok0 + (blk + 1) * 128)
                    c, sb = stat_slot()
                    sumsq(res[:, blk, :], [b_res[blk]], xnb[:], [b_xnb], c, sb)
                    rstd_from_ss(c, sb, D)
                    st, sb_ = load_rows(h_d[rows, :], src_buf=b_hd)
                    P.A("dve", "scalar_tensor_tensor", [b_res[blk], sb, b_gtile], [b_res[blk]], out=res[:, blk, :],
                        in0=res[:, blk, :], scalar=stat[:, c + 1:c + 2], in1=gtile[:], op0=ALU.mult, op1=ALU.mult)
                    P.A("dve", "tensor_tensor", [b_res[blk], sb_], [b_res[blk]], out=res[:, blk, :], in0=res[:, blk, :],
                        in1=st[:], op=ALU.add)
                    y_ops.append(P.dma("pool", y[rows, :], res[:, blk, :], [b_res[blk]], [b_y], sem_y))
                Prog.group(y_ops)
                Prog.fence(b_res, regB_bufs)
                Prog.fence([b_actT], regA_bufs)
                if s + 1 < NCH:
                    kvring.open_barrier(kv_chunk_start[s + 1])


        except _Stop:
            pass
        total_y = P.dma_cnt.get(id(sem_y), 0)
        def drain(e):
            ins = None
            for sid, sem in P.dma_sems.items():
                ins = e.wait_ge(sem, P.dma_cnt[sid])
            return ins
        P.op("pool", drain, [], [])

        with nc.Block() as block:
            P.emit(block)
    return nc


def _consts(c):
    cm = np.zeros((128, 7, 128), np.float32)
    cm[:, 0, :] = np.eye(128, dtype=np.float32)
    for i in range(128):
        d = i % 64
        if d < 8:
            cm[i + 8, 1, i] = 1.0
        elif d < 16:
            cm[i - 8, 1, i] = 1.0
    kk = np.arange(128)[:, None]
    qq = np.arange(128)[None, :]
    for wm in range(4):
        if wm < c:
            cm[:, 2 + wm, :] = 1.0
        elif wm == c:
            cm[:, 2 + wm, :] = (qq >= kk).astype(np.float32)
    cm[:, 6, :] = (qq <= kk).astype(np.float32)
    inv_freq = (np.float32(1.0) / np.power(np.float32(500000.0), np.arange(0, 16, 2, dtype=np.float32) / np.float32(16))).astype(np.float32)
    cc = np.zeros((128, 4), np.float32)
    for i in range(128):
        d = i % 64
        if d < 16:
            cc[i, 0] = inv_freq[d % 8]
        cc[i, 1] = -1.0 if d < 8 else 1.0
    return cm, cc


_PROG_CACHE = {}


def kernel(x, mem, positions, pre_mix_norm, w_in, lambda_q1, lambda_k1, lambda_q2, lambda_k2, diff_subln, sgu_v_norm,
           spatial_w, spatial_b, mem_norm, w_mem_kv, w_out, post_mix_norm, pre_ffn_norm, w_gate_up, w_down,
           post_ffn_norm):
    x = np.asarray(x); mem = np.asarray(mem); positions = np.asarray(positions)
    B, S, _ = x.shape
    NB = S // 128
    NCH = S // 2048
    f = lambda a: np.ascontiguousarray(np.asarray(a, dtype=np.float32))
    if S not in _PROG_CACHE:
        _PROG_CACHE[S] = build_program(S)
    nc = _PROG_CACHE[S]
    cpack = np.zeros((128, 1024), np.float32)
    cpack[:, 4:52] = np.stack([np.asarray(pre_mix_norm)[0].reshape(16, 128).T, np.asarray(mem_norm)[0].reshape(16, 128).T,
                               np.asarray(pre_ffn_norm)[0].reshape(16, 128).T], axis=1).reshape(128, 48)
    cpack[:, 52:56] = np.asarray(spatial_b)[0].T
    cpack[:, 64:320] = np.concatenate([np.asarray(lambda_q1)[0], np.asarray(lambda_k1)[0], np.asarray(lambda_q2)[0],
                                       np.asarray(lambda_k2)[0]])[None, :]
    cpack[:, 320:448] = np.asarray(diff_subln)[0][None, :]
    cpack[:, 448:960] = np.asarray(sgu_v_norm)[0][None, :]
    shared = {
        "spw": f(np.asarray(spatial_w)[0].transpose(1, 0, 2)),
        "g_post": f(post_mix_norm),
        "g_postffn": f(post_ffn_norm), "w_in": f(np.asarray(w_in)[0]), "w_memkv": f(np.asarray(w_mem_kv)[0]),
        "w_out": f(np.asarray(w_out)[0]), "w_gu": f(np.asarray(w_gate_up)[0]), "w_down": f(np.asarray(w_down)[0]),
    }
    in_maps = []
    idxs = []
    for core in range(8):
        b, c = core // 4, core % 4
        blk = np.array([16 * s + 4 * r + c for s in range(NCH) for r in range(4)])
        idxs.append((b, blk))
        cm, cc = _consts(c)
        xb = x[b].reshape(NB, 128, D)
        pb = positions[b].reshape(NB, 128)
        m = dict(shared)
        m.update({
            "xq": f(xb[blk].reshape(-1, D)), "xkv": f(x[b]), "posq": np.ascontiguousarray(pb[blk].reshape(1, -1).astype(np.int32)),
            "poskv": np.ascontiguousarray(positions[b].reshape(1, -1).astype(np.int32)), "mem": f(mem[b]),
            "cmat": cm,
        })
        cpk = cpack.copy()
        cpk[:, 0:4] = cc
        m["cpack"] = cpk
        in_maps.append(m)
    res = run_bass_kernel_spmd(nc, in_maps, core_ids=list(range(8)))
    out = np.empty((B, S, D), np.float32)
    for core in range(8):
        b, blk = idxs[core]
        out[b].reshape(NB, 128, D)[blk] = np.asarray(res.results[core]["y"]).reshape(-1, 128, D)
    return out
```
